# Optimizing a Trainium2 kernel written in Bass

```python
import math
import jax, jax.numpy as jnp
from jax import lax
import numpy as np

D_MODEL = 1024
BATCH = 8
SEQ = 2048
DEPTH = 4
DEC_BATCH = 128
DEC_SEQ = 8
PAST_LEN = 16384
PAGE_SIZE = 128

N_MIXERS = 2
N_SSD_LAYERS = (DEPTH + N_MIXERS - 1) // N_MIXERS
N_S5_LAYERS = DEPTH // N_MIXERS
SSD_EXPAND = 2
D_INNER = SSD_EXPAND * D_MODEL
HEAD_DIM = 64
N_HEADS = D_INNER // HEAD_DIM
N_GROUPS = 8
HEADS_PER_GROUP = N_HEADS // N_GROUPS
D_STATE = 128
CONV_K = 4
CONV_DIM = D_INNER + 2 * N_GROUPS * D_STATE
D_IN_PROJ = 2 * D_INNER + 2 * N_GROUPS * D_STATE + N_HEADS
CHUNK = 128
S5_GROUP = 16
S5_N_GROUPS = D_MODEL // S5_GROUP
S5_STATE = 64
D_FF = -(-8 * D_MODEL // (3 * 256)) * 256
RMS_EPS = 1e-6

kernel_name = 'ssd_s5_hybrid_adaln_step'


def rms(x):
    xf = x.astype(jnp.float32)
    return (xf * lax.rsqrt(jnp.mean(xf * xf, axis=-1, keepdims=True) + RMS_EPS)).astype(x.dtype)


def modulate(x, shift, scale):
    return rms(x) * (1.0 + scale[:, None, :]) + shift[:, None, :]


def ssd_scan(x, dt, A, B, C, s0):
    f32 = jnp.float32
    b, T = x.shape[:2]
    L = CHUNK if T % CHUNK == 0 else T
    nc = T // L
    x, dt, B, C, s0 = (t.astype(f32) for t in (x, dt, B, C, s0))
    xd = x * dt[..., None]
    a = dt * A
    ch = lambda t: t.reshape((b, nc, L) + t.shape[2:])
    xd, a, Bc, Cc = ch(xd), ch(a), ch(B), ch(C)
    a_cs = jnp.cumsum(a, axis=2)
    causal = jnp.tril(jnp.ones((L, L), bool))[:, :, None, None]
    seg = a_cs[:, :, :, None] - a_cs[:, :, None, :]
    decay = jnp.exp(jnp.where(causal, seg, -jnp.inf))
    CB = jnp.einsum('bclgn,bcsgn->bclsg', Cc, Bc)
    y_diag = jnp.einsum('bclsg,bclsgr,bcsgrp->bclgrp', CB, decay, xd)
    decay_to_end = jnp.exp(a_cs[:, :, -1:] - a_cs)
    chunk_states = jnp.einsum('bclgn,bclgr,bclgrp->bcgrpn', Bc, decay_to_end, xd)
    chunk_decay = jnp.exp(a_cs[:, :, -1])

    def step(s, inp):
        dec, st = inp
        return dec[..., None, None] * s + st, s

    s_final, s_in = lax.scan(step, s0, (jnp.moveaxis(chunk_decay, 1, 0), jnp.moveaxis(chunk_states, 1, 0)))
    s_in = jnp.moveaxis(s_in, 0, 1)
    y_off = jnp.einsum('bclgn,bcgrpn,bclgr->bclgrp', Cc, s_in, jnp.exp(a_cs))
    y = (y_diag + y_off).reshape(b, T, N_GROUPS, HEADS_PER_GROUP, HEAD_DIM)
    return y, s_final


def ssd_mixer(h, ssm0, conv0, in_w, conv_w, conv_b, dt_bias, A_log, Dskip, norm_w, out_w):
    b, T, _ = h.shape
    zxbcdt = h @ in_w
    z, xbc, dt = jnp.split(zxbcdt, [D_INNER, D_INNER + CONV_DIM], axis=-1)
    full = jnp.concatenate([conv0.astype(xbc.dtype), xbc], axis=1)
    new_conv = full[:, -(CONV_K - 1):]
    xbc = lax.conv_general_dilated(full, conv_w[:, None, :].astype(full.dtype), (1,), 'VALID',
                                   dimension_numbers=('NWC', 'WIO', 'NWC'),
                                   feature_group_count=CONV_DIM) + conv_b
    xbc = jax.nn.silu(xbc)
    xs, Bm, Cm = jnp.split(xbc, [D_INNER, D_INNER + N_GROUPS * D_STATE], axis=-1)
    xs = xs.reshape(b, T, N_GROUPS, HEADS_PER_GROUP, HEAD_DIM)
    Bm = Bm.reshape(b, T, N_GROUPS, D_STATE)
    Cm = Cm.reshape(b, T, N_GROUPS, D_STATE)
    dt = jax.nn.softplus((dt + dt_bias).astype(jnp.float32)).reshape(b, T, N_GROUPS, HEADS_PER_GROUP)
    A = -jnp.exp(A_log.astype(jnp.float32)).reshape(N_GROUPS, HEADS_PER_GROUP)
    s0 = ssm0.reshape(b, N_GROUPS, HEADS_PER_GROUP, HEAD_DIM, D_STATE)
    y, s_new = ssd_scan(xs, dt, A, Bm, Cm, s0)
    y = y + Dskip.reshape(N_GROUPS, HEADS_PER_GROUP)[..., None] * xs
    y = y.reshape(b, T, D_INNER) * jax.nn.silu(z)
    y = rms(y.reshape(b, T, N_GROUPS, D_INNER // N_GROUPS)).reshape(b, T, D_INNER) * norm_w
    out = (y @ out_w).astype(h.dtype)
    return out, s_new.reshape(b, N_HEADS, HEAD_DIM, D_STATE).astype(ssm0.dtype), new_conv.astype(conv0.dtype)


def s5_mixer(h, s0_re, s0_im, A_re, A_im, log_dt, B_re, B_im, C_re, C_im, Dskip, glu_w):
    f32 = jnp.float32
    b, T, _ = h.shape
    u = h.astype(f32).reshape(b, T, S5_N_GROUPS, S5_GROUP)
    dt = jnp.exp(log_dt.astype(f32))[:, None]
    lre, lim = A_re.astype(f32), A_im.astype(f32)
    mag = jnp.exp(lre * dt)
    ab_re, ab_im = mag * jnp.cos(lim * dt), mag * jnp.sin(lim * dt)
    den = lre * lre + lim * lim
    nr, ni = ab_re - 1.0, ab_im
    q_re = (nr * lre + ni * lim) / den
    q_im = (ni * lre - nr * lim) / den
    Br, Bi = B_re.astype(f32), B_im.astype(f32)
    bb_re = q_re[..., None] * Br - q_im[..., None] * Bi
    bb_im = q_re[..., None] * Bi + q_im[..., None] * Br
    bu_re = jnp.einsum('btgi,gni->btgn', u, bb_re)
    bu_im = jnp.einsum('btgi,gni->btgn', u, bb_im)
    s0r, s0i = s0_re.astype(f32), s0_im.astype(f32)
    bu_re = bu_re.at[:, 0].add(ab_re * s0r - ab_im * s0i)
    bu_im = bu_im.at[:, 0].add(ab_re * s0i + ab_im * s0r)
    a_re = jnp.broadcast_to(ab_re, bu_re.shape)
    a_im = jnp.broadcast_to(ab_im, bu_im.shape)

    def combine(e1, e2):
        a1r, a1i, b1r, b1i = e1
        a2r, a2i, b2r, b2i = e2
        return (a2r * a1r - a2i * a1i, a2r * a1i + a2i * a1r,
                a2r * b1r - a2i * b1i + b2r, a2r * b1i + a2i * b1r + b2i)

    _, _, xr, xi = lax.associative_scan(combine, (a_re, a_im, bu_re, bu_im), axis=1)
    y = jnp.einsum('btgn,gin->btgi', xr, C_re.astype(f32)) - jnp.einsum('btgn,gin->btgi', xi, C_im.astype(f32))
    y = y.reshape(b, T, D_MODEL) + Dskip.astype(f32) * h.astype(f32)
    y = jax.nn.gelu(y)
    ga, gb = jnp.split(y @ glu_w.astype(f32), 2, axis=-1)
    out = (ga * jax.nn.sigmoid(gb)).astype(h.dtype)
    return out, xr[:, -1].astype(s0_re.dtype), xi[:, -1].astype(s0_im.dtype)


def swiglu(h, w_in, w_out):
    g, u = jnp.split(h @ w_in, 2, axis=-1)
    return (jax.nn.silu(g) * u) @ w_out


def trunk(x, c, ssm0, conv0, re0, im0, p):
    silu_c = jax.nn.silu(c)
    ssm_new, conv_new, re_new, im_new = [], [], [], []
    for i in range(DEPTH):
        mod = silu_c @ p['ada_w'][i] + p['ada_b'][i]
        sh1, sc1, g1, sh2, sc2, g2 = jnp.split(mod, 6, axis=-1)
        h = modulate(x, sh1, sc1)
        j = i // N_MIXERS
        if i % N_MIXERS == 0:
            out, s, cv = ssd_mixer(h, ssm0[j], conv0[j], p['ssd_in_w'][j], p['ssd_conv_w'][j], p['ssd_conv_b'][j],
                                   p['ssd_dt_bias'][j], p['ssd_A_log'][j], p['ssd_D'][j], p['ssd_norm_w'][j],
                                   p['ssd_out_w'][j])
            ssm_new.append(s)
            conv_new.append(cv)
        else:
            out, sr, si = s5_mixer(h, re0[j], im0[j], p['s5_A_re'][j], p['s5_A_im'][j], p['s5_log_dt'][j],
                                   p['s5_B_re'][j], p['s5_B_im'][j], p['s5_C_re'][j], p['s5_C_im'][j],
                                   p['s5_D'][j], p['s5_glu_w'][j])
            re_new.append(sr)
            im_new.append(si)
        x = x + g1[:, None, :] * out
        h = modulate(x, sh2, sc2)
        x = x + g2[:, None, :] * swiglu(h, p['ffn_w_in'][i], p['ffn_w_out'][i])
    y = rms(x) * p['final_norm_w']
    return y, jnp.stack(ssm_new), jnp.stack(conv_new), jnp.stack(re_new), jnp.stack(im_new)


def setup_inputs(seed: int = 0) -> dict:
    key = jax.random.key(seed)
    ks = iter(jax.random.split(key, 48))
    f32 = jnp.float32
    nrm = lambda shape, s: jax.random.normal(next(ks), shape, f32) * s
    uni = lambda shape, lo, hi: jax.random.uniform(next(ks), shape, f32, lo, hi)
    dt0 = jnp.exp(uni((N_SSD_LAYERS, N_HEADS), math.log(1e-3), math.log(1e-1)))
    dt_bias = dt0 + jnp.log(-jnp.expm1(-dt0))
    a_im0 = jnp.pi * jnp.arange(S5_STATE, dtype=f32)
    return {
        'x_prompt': nrm((BATCH, SEQ, D_MODEL), 1.0),
        'x_sample': nrm((DEC_BATCH, DEC_SEQ, D_MODEL), 1.0),
        'state_ssm': nrm((N_SSD_LAYERS, DEC_BATCH, N_HEADS, HEAD_DIM, D_STATE), 0.3),
        'state_conv': nrm((N_SSD_LAYERS, DEC_BATCH, CONV_K - 1, CONV_DIM), 1.0),
        'state_s5_re': nrm((N_S5_LAYERS, DEC_BATCH, S5_N_GROUPS, S5_STATE), 0.3),
        'state_s5_im': nrm((N_S5_LAYERS, DEC_BATCH, S5_N_GROUPS, S5_STATE), 0.3),
        'c_prompt': nrm((BATCH, D_MODEL), 1.0),
        'c_sample': nrm((DEC_BATCH, D_MODEL), 1.0),
        'ada_w': nrm((DEPTH, D_MODEL, 6 * D_MODEL), 0.5 * D_MODEL ** -0.5),
        'ada_b': nrm((DEPTH, 6 * D_MODEL), 0.1),
        'ssd_in_w': nrm((N_SSD_LAYERS, D_MODEL, D_IN_PROJ), D_MODEL ** -0.5),
        'ssd_conv_w': nrm((N_SSD_LAYERS, CONV_K, CONV_DIM), CONV_K ** -0.5),
        'ssd_conv_b': nrm((N_SSD_LAYERS, CONV_DIM), 0.02),
        'ssd_dt_bias': dt_bias,
        'ssd_A_log': jnp.log(uni((N_SSD_LAYERS, N_HEADS), 1.0, 16.0)),
        'ssd_D': 1.0 + nrm((N_SSD_LAYERS, N_HEADS), 0.1),
        'ssd_norm_w': 1.0 + nrm((N_SSD_LAYERS, D_INNER), 0.05),
        'ssd_out_w': nrm((N_SSD_LAYERS, D_INNER, D_MODEL), D_INNER ** -0.5),
        's5_A_re': -0.5 + nrm((N_S5_LAYERS, S5_N_GROUPS, S5_STATE), 0.01),
        's5_A_im': a_im0 + nrm((N_S5_LAYERS, S5_N_GROUPS, S5_STATE), 0.01),
        's5_log_dt': uni((N_S5_LAYERS, S5_N_GROUPS), math.log(1e-3), math.log(1e-1)),
        's5_B_re': nrm((N_S5_LAYERS, S5_N_GROUPS, S5_STATE, S5_GROUP), (2 * S5_GROUP) ** -0.5),
        's5_B_im': nrm((N_S5_LAYERS, S5_N_GROUPS, S5_STATE, S5_GROUP), (2 * S5_GROUP) ** -0.5),
        's5_C_re': nrm((N_S5_LAYERS, S5_N_GROUPS, S5_GROUP, S5_STATE), S5_STATE ** -0.5),
        's5_C_im': nrm((N_S5_LAYERS, S5_N_GROUPS, S5_GROUP, S5_STATE), S5_STATE ** -0.5),
        's5_D': nrm((N_S5_LAYERS, D_MODEL), 0.5),
        's5_glu_w': nrm((N_S5_LAYERS, D_MODEL, 2 * D_MODEL), D_MODEL ** -0.5),
        'ffn_w_in': nrm((DEPTH, D_MODEL, 2 * D_FF), D_MODEL ** -0.5),
        'ffn_w_out': nrm((DEPTH, D_FF, D_MODEL), D_FF ** -0.5),
        'final_norm_w': 1.0 + nrm((D_MODEL,), 0.05),
    }


def reference(x_prompt, x_sample, state_ssm, state_conv, state_s5_re, state_s5_im, c_prompt, c_sample,
              ada_w, ada_b, ssd_in_w, ssd_conv_w, ssd_conv_b, ssd_dt_bias, ssd_A_log, ssd_D, ssd_norm_w,
              ssd_out_w, s5_A_re, s5_A_im, s5_log_dt, s5_B_re, s5_B_im, s5_C_re, s5_C_im, s5_D, s5_glu_w,
              ffn_w_in, ffn_w_out, final_norm_w):
    p = dict(ada_w=ada_w, ada_b=ada_b, ssd_in_w=ssd_in_w, ssd_conv_w=ssd_conv_w, ssd_conv_b=ssd_conv_b,
             ssd_dt_bias=ssd_dt_bias, ssd_A_log=ssd_A_log, ssd_D=ssd_D, ssd_norm_w=ssd_norm_w,
             ssd_out_w=ssd_out_w, s5_A_re=s5_A_re, s5_A_im=s5_A_im, s5_log_dt=s5_log_dt, s5_B_re=s5_B_re,
             s5_B_im=s5_B_im, s5_C_re=s5_C_re, s5_C_im=s5_C_im, s5_D=s5_D, s5_glu_w=s5_glu_w,
             ffn_w_in=ffn_w_in, ffn_w_out=ffn_w_out, final_norm_w=final_norm_w)
    b = x_prompt.shape[0]
    dtp = state_ssm.dtype
    ssm0_p = jnp.zeros((N_SSD_LAYERS, b, N_HEADS, HEAD_DIM, D_STATE), dtp)
    conv0_p = jnp.zeros((N_SSD_LAYERS, b, CONV_K - 1, CONV_DIM), state_conv.dtype)
    re0_p = jnp.zeros((N_S5_LAYERS, b, S5_N_GROUPS, S5_STATE), state_s5_re.dtype)
    im0_p = jnp.zeros((N_S5_LAYERS, b, S5_N_GROUPS, S5_STATE), state_s5_im.dtype)
    y_prompt, ssm_p, conv_p, re_p, im_p = trunk(x_prompt, c_prompt, ssm0_p, conv0_p, re0_p, im0_p, p)
    y_sample, ssm_s, conv_s, re_s, im_s = trunk(x_sample, c_sample, state_ssm, state_conv,
                                                state_s5_re, state_s5_im, p)
    return (y_prompt, y_sample, ssm_p, conv_p, re_p, im_p, ssm_s, conv_s, re_s, im_s)
```

```python
import os
import math
import numpy as np
from contextlib import ExitStack
import concourse.bass as bass
import concourse.mybir as mybir
from concourse.bass_utils import run_bass_kernel_spmd

F32 = mybir.dt.float32
BF16 = mybir.dt.bfloat16
AF = mybir.ActivationFunctionType
ALU = mybir.AluOpType

NCORES = 8
D = 1024
KC = 8
TP = 2048
NS = 16
TS = 8
NT = TP + NS * TS
TT = [(0, 512), (512, 512), (1024, 512), (1536, 512), (2048, 128)]
DEPTH = 4
DFF = 2816
FJ = DFF // 128
DIN = 2048
NH = 32
HP = 64
NG = 8
DS = 128
CONVD = 4096
DINP = 6176
EPS = 1e-6
STAGE = os.environ.get("KSTAGE", "full")
KGRP = int(os.environ.get("KGRP", "8"))
KTT = int(os.environ.get("KTT", "5"))
KSKIP = os.environ.get("KSKIP", "")
SHRINK = ("ffn_w_in", "ffn_w_out", "s5BT", "s5CT", "s5_glu_w") if STAGE == "ssd1" else ()


import types


def freeze(fn):
    if fn.__closure__ is None:
        return fn
    cells = []
    for c in fn.__closure__:
        try:
            cells.append(types.CellType(c.cell_contents))
        except ValueError:
            cells.append(c)
    return types.FunctionType(fn.__code__, fn.__globals__, fn.__name__, fn.__defaults__, tuple(cells))


class Clock:
    def __init__(self, sem, name):
        self.sem = sem
        self.count = 0
        self.name = name


class Buf:
    __slots__ = ("name", "last_w", "readers", "excl")

    def __init__(self, name="", excl=False):
        self.name = name
        self.last_w = None
        self.readers = []
        self.excl = excl


class Eng:
    def __init__(self, fw, eng, sem, name, selfsync):
        self.fw = fw
        self.eng = eng
        self.clock = Clock(sem, name)
        self.seen = {}
        self.name = name
        self.prog = []
        self.selfsync = selfsync

    def need(self, ev, waits):
        if ev is None:
            return
        clock, val = ev
        if clock is self.clock and not self.selfsync:
            return
        if self.seen.get(clock, 0) >= val:
            return
        waits.append((clock.sem, val))
        self.seen[clock] = val

    def deps(self, reads, writes, waits):
        for b in reads:
            self.need(b.last_w, waits)
            if b.excl:
                for r in b.readers:
                    if r[0] is not self.clock:
                        self.need(r, waits)
        for b in writes:
            self.need(b.last_w, waits)
            for r in b.readers:
                self.need(r, waits)

    def _record(self, ev, reads, writes):
        for b in reads:
            b.readers.append(ev)
            if len(b.readers) > 64:
                b.readers = b.readers[-64:]
        for b in writes:
            b.last_w = ev
            b.readers = []

    def op(self, fn, reads=(), writes=()):
        waits = []
        fn = freeze(fn)
        self.deps(reads, writes, waits)
        self.clock.count += 1
        sem = self.clock.sem

        def emit(e, waits=waits, fn=fn, sem=sem):
            for (s, v) in waits:
                e.wait_ge(s, v)
            r = fn(e)
            if isinstance(r, (list, tuple)):
                r = r[-1]
            r.then_inc(sem, 1)
        self.prog.append(emit)
        ev = (self.clock, self.clock.count)
        self._record(ev, reads, writes)
        return ev

    def dma(self, out, in_, reads=(), writes=()):
        fw = self.fw
        slots = fw.dma_slots[self.name]
        slot = slots[fw.dma_rr[self.name] % len(slots)]
        fw.dma_rr[self.name] += 1
        waits = []
        self.deps(reads, writes, waits)
        if slot.count > 0:
            self.need((slot, slot.count), waits)
        slot.count += 16
        sem = slot.sem

        def emit(e, waits=waits, out=out, in_=in_, sem=sem):
            for (s, v) in waits:
                e.wait_ge(s, v)
            e.dma_start(out=out, in_=in_).then_inc(sem, 16)
        self.prog.append(emit)
        ev = (slot, slot.count)
        self._record(ev, reads, writes)
        return ev

    def wait_all(self, evs):
        waits = []
        for ev in evs:
            self.need(ev, waits)

        def emit(e, waits=waits):
            for (s, v) in waits:
                e.wait_ge(s, v)
        self.prog.append(emit)


class FW:
    def __init__(self, nc, stack, n_sp=10, n_pool=6):
        self.nc = nc
        mk = lambda n: stack.enter_context(nc.semaphore(n))
        self.pe = Eng(self, nc.tensor, mk("s_pe"), "pe", False)
        self.act = Eng(self, nc.scalar, mk("s_act"), "act", True)
        self.dve = Eng(self, nc.vector, mk("s_dve"), "dve", True)
        self.pool = Eng(self, nc.gpsimd, mk("s_pool"), "pool", True)
        self.sp = Eng(self, nc.sync, mk("s_sp"), "sp", False)
        self.dma_slots = {"sp": [Clock(mk(f"s_dsp{i}"), f"dsp{i}") for i in range(n_sp)],
                          "pool": [Clock(mk(f"s_dpl{i}"), f"dpl{i}") for i in range(n_pool)]}
        self.dma_rr = {"sp": 0, "pool": 0}

    def barrier(self):
        engs = (self.pe, self.act, self.dve, self.pool, self.sp)
        evs = []
        for k in self.dma_slots:
            evs += [(s, s.count) for s in self.dma_slots[k] if s.count]
        evs += [(o.clock, o.clock.count) for o in engs if o.clock.count]
        for e in engs:
            e.wait_all([ev for ev in evs if ev[0] is not e.clock])

    def finish(self):
        evs = []
        for k in self.dma_slots:
            evs += [(s, s.count) for s in self.dma_slots[k] if s.count]
        evs += [(o.clock, o.clock.count) for o in (self.pe, self.act, self.dve, self.pool) if o.clock.count]
        self.sp.wait_all(evs)

    def emit_all(self, blk):
        for eng, attr in ((self.sp, "sync"), (self.pool, "gpsimd"), (self.pe, "tensor"),
                          (self.act, "scalar"), (self.dve, "vector")):
            def sec(e, eng=eng):
                for f in eng.prog:
                    f(e)
            getattr(blk, attr)(sec)


def sbap(t, off, dims):
    base = t if isinstance(t, bass.AP) else t[:]
    pstep = base.ap[0][0]
    npart = base.ap[0][1]
    return bass.AP(base.tensor, base.offset + off, [[pstep, npart]] + [list(d) for d in dims])


def build_program():
    nc = bass.Bass("TRN2", target_bir_lowering=False)
    dr = {}

    def din(name, shape, dt=F32):
        if name in SHRINK:
            shape = [1] * len(shape)
        dr[name] = nc.dram_tensor(name, list(shape), dt, kind="ExternalInput").ap()
        return dr[name]

    def dout(name, shape, dt=F32):
        dr[name] = nc.dram_tensor(name, list(shape), dt, kind="ExternalOutput").ap()
        return dr[name]

    xin = din("xin", [NT, D])
    cin = din("cin", [17, D])
    consts = din("consts", [8, 128, 128])
    ada_w = din("ada_w", [DEPTH, D, 6 * D])
    ada_bT = din("ada_bT", [DEPTH, 128, 48])
    ffn_w_in = din("ffn_w_in", [DEPTH, D, 2 * DFF])
    ffn_w_out = din("ffn_w_out", [DEPTH, DFF, D])
    fnwT = din("fnwT", [128, KC])
    yout = dout("yout", [NT, D])
    tidx = din("tidx", [1, NT])
    s5BT = din("s5BT", [2, 8, 128, 1024])
    s5CT = din("s5CT", [2, 32, 128, 256])
    s5col = din("s5col", [2, 128, 96])
    s5Dt = din("s5Dt", [2, 128, KC])
    s5x0 = din("s5x0", [2, 2, 128, 512])
    s5_glu_w = din("s5_glu_w", [2, D, 2 * D])
    s5out = dout("s5out", [2, 2, 17, 4096])
    ssd_in_w = din("ssd_in_w", [2, D, DINP])
    ssd_out_w = din("ssd_out_w", [2, DIN, D])
    ssd_cwT = din("ssd_cwT", [2, 128, 32, 4])
    ssd_cbT = din("ssd_cbT", [2, 128, 32])
    ssd_row = din("ssd_row", [2, 3, 32])
    ssd_norm_w = din("ssd_norm_w", [2, DIN])
    st_ssm = din("st_ssm", [2, NS, NH, HP, DS])
    st_conv = din("st_conv", [2, NS, 3, CONVD])
    ssm_p = dout("ssm_p", [2, NH, HP, DS])
    conv_p = dout("conv_p", [2, 3, CONVD])
    ssm_s = dout("ssm_s", [2, NS, NH, HP, DS])
    conv_s = dout("conv_s", [2, NS, 3, CONVD])

    with ExitStack() as st:
        fw = FW(nc, st)
        pe, act, dve, pool, sp = fw.pe, fw.act, fw.dve, fw.pool, fw.sp
        _n = [0]

        def sb(shape, dt, name=None):
            _n[0] += 1
            return st.enter_context(nc.sbuf_tensor(name or f"t{_n[0]}", list(shape), dt))

        banks = [st.enter_context(nc.psum_tensor(f"pb{i}", [128, 512], F32)) for i in range(8)]
        bankb = [Buf(f"pb{i}", excl=True) for i in range(8)]

        XT = sb([128, KC, NT], F32, "XT")
        XTb = [[Buf(f"XT{k}_{t}") for t in range(5)] for k in range(KC)]
        HT = sb([128, KC, NT], BF16, "HT")
        HTb = [[Buf(f"HT{k}_{t}") for t in range(5)] for k in range(KC)]
        identf = sb([128, 128], F32, "identf")
        identb = sb([128, 128], BF16, "identb")
        onesb = sb([128, 128], BF16, "onesb")
        b_ident = Buf("ident")
        b_ones = Buf("ones")
        modT = sb([128, 48, 17], F32, "modT")
        b_mod = Buf("modT")
        siluc = sb([128, KC, 17], BF16, "siluc")
        b_siluc = Buf("siluc")
        fnw = sb([128, KC], F32, "fnw")
        b_fnw = Buf("fnw")

        N16, N32 = 20480, 10240
        A16 = sb([128, N16], BF16, "A16")
        A32 = sb([128, N32], F32, "A32")

        class Arena:
            def __init__(self, t, n, dt):
                self.t, self.n, self.dt, self.off = t, n, dt, 0

            def reset(self):
                self.off = 0

            def take(self, *free):
                n = int(np.prod(free))
                assert self.off + n <= self.n, ("arena overflow", self.off, n, self.n)
                ap = self.t[:, self.off:self.off + n]
                self.off += n
                if len(free) == 2:
                    ap = ap.rearrange("p (a b) -> p a b", a=free[0])
                elif len(free) == 3:
                    ap = ap.rearrange("p (a b c) -> p a b c", a=free[0], b=free[1])
                return ap

            def take_f32(self, *free):
                n = int(np.prod(free)) * 2
                assert self.dt == BF16 and self.off + n <= self.n
                ap = self.t[:, self.off:self.off + n].bitcast(F32)
                self.off += n
                if len(free) == 2:
                    ap = ap.rearrange("p (a b) -> p a b", a=free[0])
                return ap

        ar16 = Arena(A16, N16, BF16)
        ar32 = Arena(A32, N32, F32)

        def new_phase():
            fw.barrier()
            ar16.reset()
            ar32.reset()

        sp.dma(identf[:], consts[0], writes=[b_ident])
        sp.dma(fnw[:], fnwT, writes=[b_fnw])
        dve.op(lambda e: e.tensor_copy(out=identb[:], in_=identf[:]), reads=[b_ident], writes=[b_ident])
        pool.op(lambda e: e.memset(onesb[:], 1.0 / D), writes=[b_ones])

        xld = [ar32.take(D) for i in range(2)]
        xldb = [Buf(f"xld{i}") for i in range(2)]
        for ti in range(NT // 128):
            s = ti % 2
            sp.dma(xld[s], xin[ti * 128:(ti + 1) * 128, :], writes=[xldb[s]])
            tt = min(ti // 4, 4)
            for half in range(2):
                bk = (ti * 2 + half) % 2
                def tr(e, s=s, half=half, bk=bk):
                    r = []
                    for q in range(4):
                        k = half * 4 + q
                        r.append(e.transpose(banks[bk][:, q * 128:(q + 1) * 128], xld[s][:, k * 128:(k + 1) * 128], identf[:]))
                    return r
                pe.op(tr, reads=[xldb[s], b_ident], writes=[bankb[bk]])
                eng = act if half == 0 else dve
                def ev(e, ti=ti, half=half, bk=bk, isact=(half == 0)):
                    o = XT[:, half * 4:(half + 1) * 4, ti * 128:(ti + 1) * 128]
                    i = banks[bk][:].rearrange("p (a b) -> p a b", a=4)
                    if isact:
                        return e.activation(out=o, in_=i, func=AF.Copy)
                    return e.tensor_copy(out=o, in_=i)
                eng.op(ev, reads=[bankb[bk]], writes=[XTb[k][tt] for k in range(half * 4, half * 4 + 4)])

        cld = ar32.take(D)[0:17, :]
        b_cld = Buf("cld")
        sp.dma(cld, cin, writes=[b_cld])
        def trc(e):
            r = []
            for k in range(KC):
                r.append(e.transpose(banks[2][:, k * 17:(k + 1) * 17], cld[:, k * 128:(k + 1) * 128], identf[0:17, 0:17]))
            return r
        pe.op(trc, reads=[b_cld, b_ident], writes=[bankb[2]])
        act.op(lambda e: e.activation(out=siluc[:], in_=banks[2][:, 0:KC * 17].rearrange("p (a b) -> p a b", a=KC), func=AF.Silu),
               reads=[bankb[2]], writes=[b_siluc])

        wslab = [sb([128, KC, 256], BF16, f"wslab{i}") for i in range(2)]
        wslabb = [Buf(f"wslab{i}") for i in range(2)]
        wrr = [0]

        def load_w(src_ap_fn, ncols):
            i = wrr[0] % 2
            wrr[0] += 1
            pool.dma(wslab[i][:, :, 0:ncols], src_ap_fn(), writes=[wslabb[i]])
            return wslab[i], wslabb[i]

        adab = sb([128, 48], F32, "adab")
        b_adab = Buf("adab")

        def compute_mod(l):
            sp.dma(adab[:], ada_bT[l], writes=[b_adab])
            for cs in range(24):
                w, wb = load_w(lambda: ada_w[l, :, cs * 256:(cs + 1) * 256].rearrange("(k p) c -> p k c", p=128), 256)
                bk = 2 + (cs % 2)
                def mm(e, w=w, bk=bk):
                    r = []
                    for f in range(2):
                        for k in range(KC):
                            r.append(e.matmul(banks[bk][:, f * 17:(f + 1) * 17], lhsT=w[:, k, f * 128:(f + 1) * 128],
                                              rhs=siluc[:, k, :], start=(k == 0), stop=(k == KC - 1)))
                    return r
                pe.op(mm, reads=[wb, b_siluc], writes=[bankb[bk]])
                def evm(e, cs=cs, bk=bk):
                    return e.tensor_tensor(out=modT[:, cs * 2:(cs + 1) * 2, :],
                                           in0=banks[bk][:, 0:34].rearrange("p (a b) -> p a b", a=2),
                                           in1=sbap(adab, cs * 2, [[1, 2], [0, 17]]), op=ALU.add)
                dve.op(evm, reads=[bankb[bk], b_adab], writes=[b_mod])
            for s0 in (8, 32):
                dve.op(lambda e, s0=s0: e.tensor_scalar(out=modT[:, s0:s0 + 8, :], in0=modT[:, s0:s0 + 8, :], scalar1=1.0,
                                                        scalar2=None, op0=ALU.add), reads=[b_mod], writes=[b_mod])

        sq = [sb([128, 512], BF16, f"sq{i}") for i in range(2)]
        sqb = [Buf(f"sq{i}") for i in range(2)]
        rstd = sb([128, 512], F32, "rstd")
        b_rstd = Buf("rstd")
        tmpA = [sb([128, 512], F32, f"tmpA{i}") for i in range(2)]
        tmpAb = [Buf(f"tmpA{i}") for i in range(2)]
        cnt = [0]

        def rms_rstd(ti):
            t0, tn = TT[ti]
            bk = 4
            for k in range(KC):
                s = cnt[0] % 2
                cnt[0] += 1
                act.op(lambda e, k=k, s=s: e.activation(out=sq[s][:, 0:tn], in_=XT[:, k, t0:t0 + tn], func=AF.Square),
                       reads=[XTb[k][ti]], writes=[sqb[s]])
                pe.op(lambda e, k=k, s=s: e.matmul(banks[bk][:, 0:tn], lhsT=onesb[:], rhs=sq[s][:, 0:tn], start=(k == 0), stop=(k == KC - 1)),
                      reads=[sqb[s], b_ones], writes=[bankb[bk]])
            dve.op(lambda e: e.tensor_scalar(out=rstd[:, 0:tn], in0=banks[bk][:, 0:tn], scalar1=EPS, scalar2=None, op0=ALU.add),
                   reads=[bankb[bk]], writes=[b_rstd])
            act.op(lambda e: e.activation(out=rstd[:, 0:tn], in_=rstd[:, 0:tn], func=AF.Ln), reads=[b_rstd], writes=[b_rstd])
            act.op(lambda e: e.activation(out=rstd[:, 0:tn], in_=rstd[:, 0:tn], func=AF.Exp, scale=-0.5), reads=[b_rstd], writes=[b_rstd])

        def modulate(sh0, sc0):
            for ti in range(5):
                t0, tn = TT[ti]
                rms_rstd(ti)
                for k in range(KC):
                    s = cnt[0] % 2
                    cnt[0] += 1
                    dve.op(lambda e, k=k, s=s: e.tensor_tensor(out=tmpA[s][:, 0:tn], in0=XT[:, k, t0:t0 + tn], in1=rstd[:, 0:tn], op=ALU.mult),
                           reads=[XTb[k][ti], b_rstd], writes=[tmpAb[s]])
                    if ti < 4:
                        act.op(lambda e, k=k, s=s: e.activation(out=HT[:, k, t0:t0 + tn], in_=tmpA[s][:, 0:tn], func=AF.Identity,
                                                                scale=modT[:, sc0 + k, 0:1], bias=modT[:, sh0 + k, 0:1]),
                               reads=[tmpAb[s], b_mod], writes=[HTb[k][ti]])
                    else:
                        dve.op(lambda e, k=k, s=s: e.tensor_tensor(out=tmpA[s][:, 0:128].rearrange("p (a b) -> p a b", a=NS),
                                                                   in0=tmpA[s][:, 0:128].rearrange("p (a b) -> p a b", a=NS),
                                                                   in1=sbap(modT, (sc0 + k) * 17 + 1, [[1, NS], [0, TS]]), op=ALU.mult),
                               reads=[tmpAb[s], b_mod], writes=[tmpAb[s]])
                        dve.op(lambda e, k=k, s=s: e.tensor_tensor(out=HT[:, k, t0:t0 + 128].rearrange("p (a b) -> p a b", a=NS),
                                                                   in0=tmpA[s][:, 0:128].rearrange("p (a b) -> p a b", a=NS),
                                                                   in1=sbap(modT, (sh0 + k) * 17 + 1, [[1, NS], [0, TS]]), op=ALU.add),
                               reads=[tmpAb[s], b_mod], writes=[HTb[k][ti]])

        def resid_add(m, ti, bk, g0, src=None, srcb=None):
            t0, tn = TT[ti]
            if src is None:
                src, srcb = banks[bk], bankb[bk]
            if ti < 4:
                dve.op(lambda e: e.scalar_tensor_tensor(out=XT[:, m, t0:t0 + tn], in0=src[:, 0:tn], scalar=modT[:, g0 + m, 0:1],
                                                        in1=XT[:, m, t0:t0 + tn], op0=ALU.mult, op1=ALU.add),
                       reads=[srcb, b_mod, XTb[m][ti]], writes=[XTb[m][ti]])
            else:
                s = cnt[0] % 2
                cnt[0] += 1
                dve.op(lambda e, s=s: e.tensor_tensor(out=tmpA[s][:, 0:128].rearrange("p (a b) -> p a b", a=NS),
                                                      in0=src[:, 0:128].rearrange("p (a b) -> p a b", a=NS),
                                                      in1=sbap(modT, (g0 + m) * 17 + 1, [[1, NS], [0, TS]]), op=ALU.mult),
                       reads=[srcb, b_mod], writes=[tmpAb[s]])
                dve.op(lambda e, s=s: e.tensor_tensor(out=XT[:, m, t0:t0 + 128], in0=XT[:, m, t0:t0 + 128], in1=tmpA[s][:, 0:128], op=ALU.add),
                       reads=[tmpAb[s], XTb[m][ti]], writes=[XTb[m][ti]])

        sgt = [sb([128, 512], F32, f"sg{i}") for i in range(2)]
        sgb = [Buf(f"sg{i}") for i in range(2)]

        def ffn(l):
            new_phase()
            HID = ar16.take(6, NT)
            HIDb = [[Buf(f"HID{j}_{t}") for t in range(5)] for j in range(6)]
            w2 = ar16.take(6, D)
            b_w2 = Buf("w2")
            for (j0, nj) in ((0, 6), (6, 6), (12, 5), (17, 5)):
                pool.dma(w2[:, 0:nj, :], ffn_w_out[l, j0 * 128:(j0 + nj) * 128, :].rearrange("(j p) c -> p j c", p=128), writes=[b_w2])
                for jj in range(nj):
                    j = j0 + jj
                    i = wrr[0] % 2
                    wrr[0] += 1
                    pool.dma(wslab[i][:, :, 0:128], ffn_w_in[l, :, j * 128:(j + 1) * 128].rearrange("(k p) c -> p k c", p=128), writes=[wslabb[i]])
                    pool.dma(wslab[i][:, :, 128:256], ffn_w_in[l, :, DFF + j * 128:DFF + (j + 1) * 128].rearrange("(k p) c -> p k c", p=128), writes=[wslabb[i]])
                    w, wb = wslab[i], wslabb[i]
                    for ti in range(5):
                        t0, tn = TT[ti]
                        bg, bu = (0, 1) if (ti % 2 == 0) else (2, 3)
                        for (bk, c0) in ((bg, 0), (bu, 128)):
                            def mm(e, w=w, bk=bk, c0=c0):
                                return [e.matmul(banks[bk][:, 0:tn], lhsT=w[:, k, c0:c0 + 128], rhs=HT[:, k, t0:t0 + tn],
                                                 start=(k == 0), stop=(k == KC - 1)) for k in range(KC)]
                            pe.op(mm, reads=[wb] + [HTb[k][ti] for k in range(KC)], writes=[bankb[bk]])
                        s = cnt[0] % 2
                        cnt[0] += 1
                        act.op(lambda e, s=s, bg=bg: e.activation(out=sgt[s][:, 0:tn], in_=banks[bg][:, 0:tn], func=AF.Silu),
                               reads=[bankb[bg]], writes=[sgb[s]])
                        dve.op(lambda e, s=s, bu=bu, jj=jj: e.tensor_tensor(out=HID[:, jj, t0:t0 + tn], in0=sgt[s][:, 0:tn], in1=banks[bu][:, 0:tn], op=ALU.mult),
                               reads=[sgb[s], bankb[bu]], writes=[HIDb[jj][ti]])
                for m in range(KC):
                    for ti in range(5):
                        t0, tn = TT[ti]
                        bk = 4 + ((m * 5 + ti) % 3)
                        def mm2(e, m=m, bk=bk):
                            return [e.matmul(banks[bk][:, 0:tn], lhsT=w2[:, jj, m * 128:(m + 1) * 128], rhs=HID[:, jj, t0:t0 + tn],
                                             start=(jj == 0), stop=(jj == nj - 1)) for jj in range(nj)]
                        pe.op(mm2, reads=[b_w2] + [HIDb[jj][ti] for jj in range(nj)], writes=[bankb[bk]])
                        resid_add(m, ti, bk, 40)


        MAGIC = 12582912.0
        TWO_PI = 2.0 * math.pi

        def s5_layer(l, j):
            new_phase()
            TIDX = ar32.take(NT); b_tidx = Buf("tidx")
            WS = ar32.take(NT); WC = ar32.take(NT)
            b_ws = [Buf(f"ws{t}") for t in range(5)]; b_wc = [Buf(f"wc{t}") for t in range(5)]
            tmpt = [ar32.take(512) for _ in range(2)]; b_tmpt = [Buf("tmpt0"), Buf("tmpt1")]
            Craw = ar32.take(2, 128); b_craw = Buf("craw")
            Ctmp = ar32.take(2, 128); b_ctmp = Buf("ctmp")
            PRM = ar32.take(24, 32); b_prm = Buf("prm")
            X0t = ar32.take(2, 544); b_x0t = Buf("x0t")
            V = [ar16.take_f32(NT) for _ in range(2)]
            b_v = [[Buf(f"v{r}_{t}") for t in range(5)] for r in range(2)]
            Xb = [ar16.take(NT) for _ in range(2)]
            b_xb = [[Buf(f"xb{r}_{t}") for t in range(5)] for r in range(2)]
            BTs = ar16.take(2, 4, 128); b_bts = Buf("bts")
            CQ = [ar16.take(2, 128) for _ in range(2)]; b_cq = [Buf("cq0"), Buf("cq1")]
            XF = ar16.take_f32(2, 32 * 17); b_xf = Buf("xf")
            X0 = ar16.take_f32(2, 512); b_x0 = Buf("x0")
            P = lambda i: PRM[:, i, :]
            Pc = lambda i, q: PRM[:, i, q:q + 1]
            (ARE, AIM, DTC, LR, TH, RHO, SIN, COS, ABR, ABI, RDEN, NR, QRE, QIM, NQR, NQI, T1, T2, IR, II, TH2, DSK) = range(22)

            def dv(fn, reads=(b_prm,), writes=(b_prm,)):
                dve.op(fn, reads=list(reads), writes=list(writes))

            def ac(fn, reads=(b_prm,), writes=(b_prm,)):
                act.op(fn, reads=list(reads), writes=list(writes))

            sp.dma(TIDX, bass.AP(tidx.tensor, 0, [[0, 128], [1, NT]]), writes=[b_tidx])
            sp.dma(PRM[:, 0:3, :], s5col[j].rearrange("p (a b) -> p a b", a=3), writes=[b_prm])
            sp.dma(PRM[:, DSK, 0:KC], s5Dt[j], writes=[b_prm])
            sp.dma(X0, s5x0[j].rearrange("r p f -> p r f"), writes=[b_x0])
            tt_ = lambda o, a, b, op: (lambda e: e.tensor_tensor(out=o, in0=a, in1=b, op=op))
            ac(lambda e: e.activation(out=P(DTC), in_=P(DTC), func=AF.Exp))
            dv(tt_(P(LR), P(ARE), P(DTC), ALU.mult))
            dv(tt_(P(TH), P(AIM), P(DTC), ALU.mult))
            ac(lambda e: e.activation(out=P(RHO), in_=P(LR), func=AF.Exp))
            dv(lambda e: e.tensor_scalar(out=P(T1), in0=P(TH), scalar1=1.0 / TWO_PI, scalar2=MAGIC, op0=ALU.mult, op1=ALU.add))
            dv(lambda e: e.tensor_scalar(out=P(T1), in0=P(T1), scalar1=-MAGIC, scalar2=-TWO_PI, op0=ALU.add, op1=ALU.mult))
            dv(tt_(P(T1), P(T1), P(TH), ALU.add))
            ac(lambda e: e.activation(out=P(SIN), in_=P(T1), func=AF.Sin))
            ac(lambda e: e.activation(out=P(COS), in_=P(T1), func=AF.Sin, scale=0.5))
            ac(lambda e: e.activation(out=P(COS), in_=P(COS), func=AF.Square, scale=math.sqrt(2.0)))
            ac(lambda e: e.activation(out=P(COS), in_=P(COS), func=AF.Identity, scale=-1.0, bias=1.0))
            dv(lambda e: e.tensor_scalar(out=P(TH2), in0=P(TH), scalar1=1.0 / TWO_PI, scalar2=None, op0=ALU.mult))
            dv(tt_(P(ABR), P(RHO), P(COS), ALU.mult))
            dv(tt_(P(ABI), P(RHO), P(SIN), ALU.mult))
            dv(tt_(P(T1), P(ARE), P(ARE), ALU.mult))
            dv(tt_(P(T2), P(AIM), P(AIM), ALU.mult))
            dv(tt_(P(T1), P(T1), P(T2), ALU.add))
            dv(lambda e: e.reciprocal(out=P(RDEN), in_=P(T1)))
            dv(lambda e: e.tensor_scalar(out=P(NR), in0=P(ABR), scalar1=-1.0, scalar2=None, op0=ALU.add))
            dv(tt_(P(T1), P(NR), P(ARE), ALU.mult))
            dv(tt_(P(T2), P(ABI), P(AIM), ALU.mult))
            dv(tt_(P(T1), P(T1), P(T2), ALU.add))
            dv(tt_(P(QRE), P(T1), P(RDEN), ALU.mult))
            dv(tt_(P(T1), P(ABI), P(ARE), ALU.mult))
            dv(tt_(P(T2), P(NR), P(AIM), ALU.mult))
            dv(tt_(P(T1), P(T1), P(T2), ALU.subtract))
            dv(tt_(P(QIM), P(T1), P(RDEN), ALU.mult))
            dv(lambda e: e.tensor_scalar(out=P(NQR), in0=P(QRE), scalar1=-1.0, scalar2=None, op0=ALU.mult))
            dv(lambda e: e.tensor_scalar(out=P(NQI), in0=P(QIM), scalar1=-1.0, scalar2=None, op0=ALU.mult))
            dv(tt_(P(T1), P(QRE), P(QRE), ALU.mult))
            dv(tt_(P(T2), P(QIM), P(QIM), ALU.mult))
            dv(tt_(P(T1), P(T1), P(T2), ALU.add))
            dv(lambda e: e.reciprocal(out=P(T2), in_=P(T1)))
            dv(tt_(P(IR), P(QRE), P(T2), ALU.mult))
            dv(tt_(P(II), P(NQI), P(T2), ALU.mult))
            v3 = lambda ap: ap.rearrange("p (a b) -> p a b", a=32)
            bc = lambda i: sbap(PRM, i * 32, [[1, 32], [0, NS]])
            x0r, x0i = v3(X0[:, 0, :]), v3(X0[:, 1, :])
            ta, tb = v3(X0t[:, 0, 0:512]), v3(X0t[:, 1, 0:512])
            tc, td = v3(tmpt[0]), v3(tmpt[1])
            allb = [b_prm, b_x0, b_x0t, b_tmpt[0], b_tmpt[1]]
            dvx = lambda fn: dve.op(fn, reads=allb, writes=[b_x0, b_x0t, b_tmpt[0], b_tmpt[1]])
            dvx(tt_(ta, x0r, bc(IR), ALU.mult))
            dvx(tt_(tb, x0i, bc(II), ALU.mult))
            dvx(tt_(ta, ta, tb, ALU.subtract))
            dvx(tt_(tb, x0r, bc(II), ALU.mult))
            dvx(tt_(tc, x0i, bc(IR), ALU.mult))
            dvx(tt_(tb, tb, tc, ALU.add))
            dvx(tt_(tc, ta, bc(ABR), ALU.mult))
            dvx(tt_(td, tb, bc(ABI), ALU.mult))
            dvx(tt_(x0r, tc, td, ALU.subtract))
            dvx(tt_(tc, tb, bc(ABR), ALU.mult))
            dvx(tt_(td, ta, bc(ABI), ALU.mult))
            dvx(tt_(x0i, tc, td, ALU.add))

            ALLT = list(range(5))
            for cc in range(KC):
                pool.dma(BTs, s5BT[j, cc].rearrange("p (r a n) -> p r a n", r=2, a=4), writes=[b_bts])
                for pi in range(4):
                    q = cc * 4 + pi
                    cs = q % 2
                    sp.dma(Craw, s5CT[j, q].rearrange("p (r c) -> p r c", r=2), writes=[b_craw])
                    def ctab(o, sa, sb_, op):
                        pool.op(lambda e: e.tensor_scalar(out=Ctmp[:, 0, :], in0=Craw[:, 0, :], scalar1=Pc(sa, q), scalar2=None, op0=ALU.mult),
                                reads=[b_craw, b_prm], writes=[b_ctmp])
                        pool.op(lambda e: e.tensor_scalar(out=Ctmp[:, 1, :], in0=Craw[:, 1, :], scalar1=Pc(sb_, q), scalar2=None, op0=ALU.mult),
                                reads=[b_craw, b_prm], writes=[b_ctmp])
                        pool.op(lambda e: e.tensor_tensor(out=o, in0=Ctmp[:, 0, :], in1=Ctmp[:, 1, :], op=op), reads=[b_ctmp], writes=[b_cq[cs]])
                    ctab(CQ[cs][:, 0, :], QRE, QIM, ALU.subtract)
                    ctab(CQ[cs][:, 1, :], NQI, NQR, ALU.add)
                    W0, W1 = V[0], V[1]
                    dve.op(lambda e: e.tensor_scalar(out=W1, in0=TIDX, scalar1=Pc(TH2, q), scalar2=MAGIC, op0=ALU.mult, op1=ALU.add),
                           reads=[b_tidx, b_prm], writes=b_v[1])
                    dve.op(lambda e: e.tensor_scalar(out=W1, in0=W1, scalar1=-MAGIC, scalar2=-TWO_PI, op0=ALU.add, op1=ALU.mult),
                           reads=b_v[1], writes=b_v[1])
                    dve.op(lambda e: e.scalar_tensor_tensor(out=W0, in0=TIDX, scalar=Pc(TH, q), in1=W1, op0=ALU.mult, op1=ALU.add),
                           reads=[b_tidx, b_prm] + b_v[1], writes=b_v[0])
                    act.op(lambda e: e.activation(out=WS, in_=W0, func=AF.Sin), reads=b_v[0], writes=b_ws)
                    act.op(lambda e: e.activation(out=WC, in_=W0, func=AF.Sin, scale=0.5), reads=b_v[0], writes=b_wc)
                    act.op(lambda e: e.activation(out=WC, in_=WC, func=AF.Square, scale=math.sqrt(2.0)), reads=b_wc, writes=b_wc)
                    act.op(lambda e: e.activation(out=WC, in_=WC, func=AF.Identity, scale=-1.0, bias=1.0), reads=b_wc, writes=b_wc)
                    for ti in range(5):
                        t0, tn = TT[ti]
                        sl = slice(t0, t0 + tn)
                        for r in range(2):
                            pe.op(lambda e, r=r: e.matmul(banks[r][:, 0:tn], lhsT=BTs[:, r, pi, :], rhs=HT[:, cc, sl], start=True, stop=True),
                                  reads=[b_bts, HTb[cc][ti]], writes=[bankb[r]])
                        dve.op(lambda e: e.tensor_tensor(out=V[0][:, sl], in0=banks[0][:, 0:tn], in1=WC[:, sl], op=ALU.mult),
                               reads=[bankb[0], b_wc[ti], b_ws[ti]], writes=[b_v[0][ti]])
                        dve.op(lambda e: e.tensor_tensor(out=tmpt[0][:, 0:tn], in0=banks[1][:, 0:tn], in1=WS[:, sl], op=ALU.mult),
                               reads=[bankb[1], b_ws[ti]], writes=[b_tmpt[0]])
                        pool.op(lambda e: e.tensor_tensor(out=V[0][:, sl], in0=V[0][:, sl], in1=tmpt[0][:, 0:tn], op=ALU.add),
                                reads=[b_v[0][ti], b_tmpt[0]], writes=[b_v[0][ti]])
                        dve.op(lambda e: e.tensor_tensor(out=V[1][:, sl], in0=banks[1][:, 0:tn], in1=WC[:, sl], op=ALU.mult),
                               reads=[bankb[1], b_wc[ti], b_ws[ti]], writes=[b_v[1][ti]])
                        dve.op(lambda e: e.tensor_tensor(out=tmpt[1][:, 0:tn], in0=banks[0][:, 0:tn], in1=WS[:, sl], op=ALU.mult),
                               reads=[bankb[0], b_ws[ti]], writes=[b_tmpt[1]])
                        pool.op(lambda e: e.tensor_tensor(out=V[1][:, sl], in0=V[1][:, sl], in1=tmpt[1][:, 0:tn], op=ALU.subtract),
                                reads=[b_v[1][ti], b_tmpt[1]], writes=[b_v[1][ti]])
                    for r in range(2):
                        dve.op(lambda e, r=r: e.tensor_tensor_scan(out=V[r][:, 0:TP], data0=sbap(PRM, RHO * 32 + q, [[0, TP]]), data1=V[r][:, 0:TP],
                                                                  initial=0.0, op0=ALU.mult, op1=ALU.add),
                               reads=[b_prm] + b_v[r][0:4], writes=b_v[r][0:4])
                        pool.op(lambda e, r=r: e.tensor_tensor(out=V[r][:, TP:NT:TS], in0=V[r][:, TP:NT:TS], in1=X0[:, r, q * NS:(q + 1) * NS], op=ALU.add),
                                reads=[b_x0, b_v[r][4]], writes=[b_v[r][4]])
                        for t in range(1, TS):
                            dve.op(lambda e, r=r, t=t: e.scalar_tensor_tensor(out=V[r][:, TP + t:NT:TS], in0=V[r][:, TP + t - 1:NT:TS], scalar=Pc(RHO, q),
                                                                                in1=V[r][:, TP + t:NT:TS], op0=ALU.mult, op1=ALU.add),
                                    reads=[b_prm, b_v[r][4]], writes=[b_v[r][4]])
                    cols = slice(TP - 1, NT, TS)
                    ca, cb = Ctmp[:, 0, 0:17], Ctmp[:, 0, 32:49]
                    rdl = [b_v[0][3], b_v[0][4], b_v[1][3], b_v[1][4], b_ws[3], b_ws[4], b_wc[3], b_wc[4], b_ctmp]
                    pool.op(lambda e: e.tensor_tensor(out=ca, in0=V[0][:, cols], in1=WC[:, cols], op=ALU.mult), reads=rdl, writes=[b_ctmp])
                    pool.op(lambda e: e.tensor_tensor(out=cb, in0=V[1][:, cols], in1=WS[:, cols], op=ALU.mult), reads=rdl, writes=[b_ctmp])
                    pool.op(lambda e: e.tensor_tensor(out=XF[:, 0, q * 17:(q + 1) * 17], in0=ca, in1=cb, op=ALU.subtract), reads=[b_ctmp], writes=[b_xf])
                    pool.op(lambda e: e.tensor_tensor(out=ca, in0=V[0][:, cols], in1=WS[:, cols], op=ALU.mult), reads=rdl, writes=[b_ctmp])
                    pool.op(lambda e: e.tensor_tensor(out=cb, in0=V[1][:, cols], in1=WC[:, cols], op=ALU.mult), reads=rdl, writes=[b_ctmp])
                    pool.op(lambda e: e.tensor_tensor(out=XF[:, 1, q * 17:(q + 1) * 17], in0=ca, in1=cb, op=ALU.add), reads=[b_ctmp], writes=[b_xf])
                    for ti in range(5):
                        t0, tn = TT[ti]
                        sl = slice(t0, t0 + tn)
                        dve.op(lambda e: e.tensor_tensor(out=tmpt[0][:, 0:tn], in0=V[0][:, sl], in1=WS[:, sl], op=ALU.mult),
                               reads=[b_v[0][ti], b_ws[ti]], writes=[b_tmpt[0]])
                        dve.op(lambda e: e.tensor_tensor(out=tmpt[1][:, 0:tn], in0=V[1][:, sl], in1=WS[:, sl], op=ALU.mult),
                               reads=[b_v[1][ti], b_ws[ti]], writes=[b_tmpt[1]])
                        pool.op(lambda e: e.tensor_tensor(out=V[0][:, sl], in0=V[0][:, sl], in1=WC[:, sl], op=ALU.mult),
                                reads=[b_v[0][ti], b_wc[ti], b_tmpt[0]], writes=[b_v[0][ti]])
                        pool.op(lambda e: e.tensor_tensor(out=V[1][:, sl], in0=V[1][:, sl], in1=WC[:, sl], op=ALU.mult),
                                reads=[b_v[1][ti], b_wc[ti], b_tmpt[1]], writes=[b_v[1][ti]])
                        dve.op(lambda e: e.tensor_tensor(out=Xb[0][:, sl], in0=V[0][:, sl], in1=tmpt[1][:, 0:tn], op=ALU.subtract),
                               reads=[b_v[0][ti], b_tmpt[1]], writes=[b_xb[0][ti]])
                        pool.op(lambda e: e.tensor_tensor(out=Xb[1][:, sl], in0=V[1][:, sl], in1=tmpt[0][:, 0:tn], op=ALU.add),
                                reads=[b_v[1][ti], b_tmpt[0]], writes=[b_xb[1][ti]])
                        def mmc(e):
                            return [e.matmul(banks[3 + ti][:, 0:tn], lhsT=CQ[cs][:, 0, :], rhs=Xb[0][:, sl], start=(pi == 0), stop=False),
                                    e.matmul(banks[3 + ti][:, 0:tn], lhsT=CQ[cs][:, 1, :], rhs=Xb[1][:, sl], start=False, stop=(pi == 3))]
                        pe.op(mmc, reads=[b_cq[cs], b_xb[0][ti], b_xb[1][ti]], writes=[bankb[3 + ti]])
                for ti in range(5):
                    t0, tn = TT[ti]
                    sl = slice(t0, t0 + tn)
                    s = cnt[0] % 2
                    cnt[0] += 1
                    dve.op(lambda e: e.scalar_tensor_tensor(out=tmpA[s][:, 0:tn], in0=HT[:, cc, sl], scalar=PRM[:, DSK, cc:cc + 1], in1=banks[3 + ti][:, 0:tn],
                                                            op0=ALU.mult, op1=ALU.add),
                           reads=[HTb[cc][ti], b_prm, bankb[3 + ti]], writes=[tmpAb[s]])
                    act.op(lambda e: e.activation(out=HT[:, cc, sl], in_=tmpA[s][:, 0:tn], func=AF.Gelu_apprx_tanh),
                           reads=[tmpAb[s]], writes=[HTb[cc][ti]])
            for m in range(KC):
                i = wrr[0] % 2
                wrr[0] += 1
                pool.dma(wslab[i][:, :, 0:128], s5_glu_w[j, :, m * 128:(m + 1) * 128].rearrange("(k p) c -> p k c", p=128), writes=[wslabb[i]])
                pool.dma(wslab[i][:, :, 128:256], s5_glu_w[j, :, D + m * 128:D + (m + 1) * 128].rearrange("(k p) c -> p k c", p=128), writes=[wslabb[i]])
                w, wb = wslab[i], wslabb[i]
                for ti in range(5):
                    t0, tn = TT[ti]
                    ba, bb_ = (0, 1) if (ti % 2 == 0) else (2, 3)
                    for (bk, c0) in ((ba, 0), (bb_, 128)):
                        def mm(e, bk=bk, c0=c0):
                            return [e.matmul(banks[bk][:, 0:tn], lhsT=w[:, k, c0:c0 + 128], rhs=HT[:, k, t0:t0 + tn],
                                             start=(k == 0), stop=(k == KC - 1)) for k in range(KC)]
                        pe.op(mm, reads=[wb] + [HTb[k][ti] for k in range(KC)], writes=[bankb[bk]])
                    s = cnt[0] % 2
                    cnt[0] += 1
                    act.op(lambda e: e.activation(out=sgt[s][:, 0:tn], in_=banks[bb_][:, 0:tn], func=AF.Sigmoid), reads=[bankb[bb_]], writes=[sgb[s]])
                    dve.op(lambda e: e.tensor_tensor(out=sgt[s][:, 0:tn], in0=banks[ba][:, 0:tn], in1=sgt[s][:, 0:tn], op=ALU.mult),
                           reads=[sgb[s], bankb[ba]], writes=[sgb[s]])
                    resid_add(m, ti, None, 16, src=sgt[s], srcb=sgb[s])
            for r in range(2):
                for hf in range(2):
                    qs = slice(hf * 16 * 17, (hf + 1) * 16 * 17)
                    v3h = lambda ap: ap.rearrange("p (a b) -> p a b", a=16)
                    bch = lambda i: sbap(PRM, i * 32 + hf * 16, [[1, 16], [0, 17]])
                    xr, xi = v3h(XF[:, 0, qs]), v3h(XF[:, 1, qs])
                    o = v3h(X0t[:, r, qs])
                    t_ = v3h(tmpt[0][:, 0:272])
                    rd = [b_xf, b_prm, b_x0t, b_tmpt[0]]
                    if r == 0:
                        dve.op(tt_(o, xr, bch(QRE), ALU.mult), reads=rd, writes=[b_x0t])
                        dve.op(tt_(t_, xi, bch(QIM), ALU.mult), reads=rd, writes=[b_tmpt[0]])
                        dve.op(tt_(o, o, t_, ALU.subtract), reads=rd, writes=[b_x0t])
                    else:
                        dve.op(tt_(o, xr, bch(QIM), ALU.mult), reads=rd, writes=[b_x0t])
                        dve.op(tt_(t_, xi, bch(QRE), ALU.mult), reads=rd, writes=[b_tmpt[0]])
                        dve.op(tt_(o, o, t_, ALU.add), reads=rd, writes=[b_x0t])
            for r in range(2):
                for g4 in range(8):
                    bk = g4 % 8
                    def trs(e):
                        return [e.transpose(banks[bk][0:17, ql * 128:(ql + 1) * 128], X0t[:, r, (g4 * 4 + ql) * 17:(g4 * 4 + ql + 1) * 17], identf[:])
                                for ql in range(4)]
                    pe.op(trs, reads=[b_x0t, b_ident], writes=[bankb[bk]])
                    s = cnt[0] % 2
                    cnt[0] += 1
                    act.op(lambda e: e.activation(out=sgt[s][0:17, :], in_=banks[bk][0:17, :], func=AF.Copy), reads=[bankb[bk]], writes=[sgb[s]])
                    sp.dma(s5out[j, r, :, g4 * 512:(g4 + 1) * 512], sgt[s][0:17, :], reads=[sgb[s]])


        def ssd_layer(l, j):
            new_phase()
            f32t = lambda *sh: ar32.take(*sh)
            CON = f32t(7, 128); b_con = Buf("con")
            DTS, AA, ACS, ATOT, EE, DTD, CD = [f32t(17, 32) for _ in range(7)]
            b_dt = Buf("dtarrs")
            ROW = f32t(3, 32); b_row = Buf("row")
            S0 = f32t(4, 2, 128); b_s0l = [Buf(f"s0_{i}") for i in range(4)]
            RSEG = f32t(4, 128); b_rseg = Buf("rseg")
            DECT = f32t(4, 128); b_dect = Buf("dect")
            CBM = f32t(128); b_cbm = Buf("cbm")
            XSF = f32t(256); b_xsf = Buf("xsf")
            YA = f32t(256); b_ya = Buf("ya")
            YB = f32t(256); b_yb = Buf("yb")
            ZS = f32t(256); b_zs = Buf("zs")
            ST = f32t(256); b_st = Buf("st")
            NW = f32t(256); b_nw = Buf("nw")
            NCV = f32t(512); b_ncv = Buf("ncv")
            YOT = f32t(2, 128); b_yot = Buf("yot")
            ABC = f32t(2, 128); b_abc = Buf("abc")
            CDT = f32t(2, 16); b_cdt = Buf("cdt")
            STG = f32t(2, 128); b_stg = Buf("stg")
            SS = f32t(8); b_ss = Buf("ss")
            CWA = f32t(32, 4); CBA = f32t(32); b_cw = Buf("cw")
            W = ar16.take(KC, 768); b_w = Buf("w")
            OW = ar16.take(2, D); b_ow = Buf("ow")
            DG = ar16.take(4, 4, 128); b_dg = Buf("dg")
            S0T = ar16.take(4, 256); b_s0t = Buf("s0t")
            XPRE = ar16.take(4, 515); b_xpre = Buf("xpre")
            XPS = ar16.take(4 * NS, 11); b_xps = Buf("xps")
            XBC = ar16.take(4, 512); b_xbc = [Buf(f"xbc{o}") for o in range(4)]
            YT = ar16.take(2, 512); b_yt = Buf("yt")
            XD = ar16.take(4, 64); b_xd = Buf("xd")
            XDD = ar16.take(256); b_xdd = Buf("xdd")
            BTM = ar16.take(128); b_btm = Buf("btm")
            MT = ar16.take(4, 128); b_mt = Buf("mt")
            STB = ar16.take(256); b_stb = Buf("stb")
            BMB = ar16.take(128); b_bmb = Buf("bmb")
            WDT = ar16.take(KC, 32); b_wdt = Buf("wdt")
            YN = ar16.take(256); b_yn = Buf("yn")
            TRIp, Up, SAMEp, TRIs, Us, SAMEs, SEL = [CON[:, i, :] for i in range(7)]
            pbf = lambda bk: banks[bk][:].bitcast(BF16)

            sp.dma(CON, consts[1:8].rearrange("a p c -> p a c"), writes=[b_con])
            sp.dma(CWA, ssd_cwT[j], writes=[b_cw])
            sp.dma(CBA, ssd_cbT[j], writes=[b_cw])
            print("ssd arena use", ar16.off, ar32.off)
            sp.dma(ROW, bass.AP(ssd_row.tensor, j * 96, [[0, 128], [32, 3], [1, 32]]), writes=[b_row])
            pool.dma(WDT, ssd_in_w[j, :, 6144:6176].rearrange("(k p) c -> p k c", p=128), writes=[b_wdt])
            act.op(lambda e: e.activation(out=ROW[:, 1, :], in_=ROW[:, 1, :], func=AF.Exp), reads=[b_row], writes=[b_row])
            dve.op(lambda e: e.tensor_scalar(out=ROW[:, 1, :], in0=ROW[:, 1, :], scalar1=-1.0, scalar2=None, op0=ALU.mult), reads=[b_row], writes=[b_row])
            for T in range(17):
                bk, c0 = (0, T * 32) if T < 16 else (1, 0)
                def mmdt(e):
                    return [e.matmul(banks[bk][:, c0:c0 + 32], lhsT=HT[:, k, T * 128:(T + 1) * 128], rhs=WDT[:, k, :], start=(k == 0), stop=(k == KC - 1))
                            for k in range(KC)]
                pe.op(mmdt, reads=[b_wdt] + [HTb[k][min(T // 4, 4)] for k in range(KC)], writes=[bankb[bk]])
            rowb = lambda i, n: sbap(ROW, i * 32, [[0, n], [1, 32]])
            dve.op(lambda e: e.tensor_tensor(out=DTS[:, 0:16, :], in0=banks[0][:].rearrange("p (a b) -> p a b", a=16), in1=rowb(0, 16), op=ALU.add),
                   reads=[bankb[0], b_row], writes=[b_dt])
            dve.op(lambda e: e.tensor_tensor(out=DTS[:, 16:17, :], in0=banks[1][:, 0:32].rearrange("p (a b) -> p a b", a=1), in1=rowb(0, 1), op=ALU.add),
                   reads=[bankb[1], b_row], writes=[b_dt])
            act.op(lambda e: e.activation(out=DTS, in_=DTS, func=AF.Exp), reads=[b_dt], writes=[b_dt])
            act.op(lambda e: e.activation(out=DTS, in_=DTS, func=AF.Ln, bias=1.0), reads=[b_dt], writes=[b_dt])
            dve.op(lambda e: e.tensor_tensor(out=AA, in0=DTS, in1=rowb(1, 17), op=ALU.mult), reads=[b_dt, b_row], writes=[b_dt])
            for T in range(17):
                tri, same = (TRIp, SAMEp) if T < 16 else (TRIs, SAMEs)
                bk, c0 = (2, T * 32) if T < 16 else (3, 0)
                pe.op(lambda e: e.matmul(banks[bk][:, c0:c0 + 32], lhsT=tri, rhs=AA[:, T, :], start=True, stop=True),
                      reads=[b_con, b_dt], writes=[bankb[bk]])
                pe.op(lambda e: e.matmul(banks[bk + 2][:, c0:c0 + 32], lhsT=same, rhs=AA[:, T, :], start=True, stop=True),
                      reads=[b_con, b_dt], writes=[bankb[bk + 2]])
            for (dst, b0, b1) in ((ACS, 2, 3), (ATOT, 4, 5)):
                dve.op(lambda e: e.tensor_copy(out=dst[:, 0:16, :], in_=banks[b0][:].rearrange("p (a b) -> p a b", a=16)), reads=[bankb[b0]], writes=[b_dt])
                dve.op(lambda e: e.tensor_copy(out=dst[:, 16:17, :], in_=banks[b1][:, 0:32].rearrange("p (a b) -> p a b", a=1)), reads=[bankb[b1]], writes=[b_dt])
            act.op(lambda e: e.activation(out=EE, in_=ACS, func=AF.Exp), reads=[b_dt], writes=[b_dt])
            act.op(lambda e: e.activation(out=CD, in_=ATOT, func=AF.Exp), reads=[b_dt], writes=[b_dt])
            dve.op(lambda e: e.tensor_tensor(out=DTD, in0=ATOT, in1=ACS, op=ALU.subtract), reads=[b_dt], writes=[b_dt])
            act.op(lambda e: e.activation(out=DTD, in_=DTD, func=AF.Exp), reads=[b_dt], writes=[b_dt])
            dve.op(lambda e: e.tensor_tensor(out=DTD, in0=DTD, in1=DTS, op=ALU.mult), reads=[b_dt], writes=[b_dt])

            for g in range(KGRP):
                cols = ((g * 256, 256), (2048 + g * 256, 256), (4096 + g * 128, 128), (5120 + g * 128, 128))
                woff = (0, 256, 512, 640)
                for (c0, n), wo in zip(cols, woff):
                    pool.dma(W[:, :, wo:wo + n], ssd_in_w[j, :, c0:c0 + n].rearrange("(k p) c -> p k c", p=128), writes=[b_w])
                pool.dma(OW, ssd_out_w[j, g * 256:(g + 1) * 256, :].rearrange("(k p) c -> p k c", p=128), writes=[b_ow])
                chs = (2 * g, 2 * g + 1, 16 + g, 24 + g)
                ccol = ((g * 256, 256), (2048 + g * 128, 128), (3072 + g * 128, 128))
                sp.dma(NW, bass.AP(ssd_norm_w.tensor, j * DIN + g * 256, [[0, 128], [1, 256]]), writes=[b_nw])
                for o in range(4):
                    for k in range(4):
                        pool.op(lambda e: e.tensor_scalar(out=DG[:, o, k, :], in0=identb[:], scalar1=CWA[:, chs[o], k:k + 1], scalar2=None, op0=ALU.mult),
                                reads=[b_ident, b_cw], writes=[b_dg])
                pool.op(lambda e: e.memset(XPRE[:, :, 0:3], 0.0), writes=[b_xpre])
                pool.op(lambda e: e.memset(ST, 0.0), writes=[b_st])
                pool.op(lambda e: e.memset(STB, 0.0), writes=[b_stb])
                for (c0, n), wo in zip(ccol, (0, 256, 384)):
                    sp.dma(NCV[0:48, wo:wo + n], st_conv[j, :, :, c0:c0 + n].rearrange("b k c -> (b k) c"), writes=[b_ncv])
                pe.op(lambda e: [e.transpose(banks[0][:, o * 48:(o + 1) * 48], NCV[0:48, o * 128:(o + 1) * 128], identf[0:48, 0:48]) for o in range(4)],
                      reads=[b_ncv, b_ident], writes=[bankb[0]])
                for o in range(4):
                    act.op(lambda e: e.activation(out=XPS[:, o * NS:(o + 1) * NS, 0:3], in_=banks[0][:, o * 48:(o + 1) * 48].rearrange("p (a b) -> p a b", a=NS), func=AF.Copy),
                           reads=[bankb[0]], writes=[b_xps])

                for tt in range(KTT):
                    t0, tn = TT[tt]
                    samp = (tt == 4)
                    for o in range(4):
                        bk = o % 2
                        pe.op(lambda e: [e.matmul(banks[bk][:, 0:tn], lhsT=W[:, k, 256 + o * 128:256 + (o + 1) * 128], rhs=HT[:, k, t0:t0 + tn],
                                                   start=(k == 0), stop=(k == KC - 1)) for k in range(KC)],
                              reads=[b_w] + [HTb[k][tt] for k in range(KC)], writes=[bankb[bk]])
                        if not samp:
                            act.op(lambda e: e.activation(out=XPRE[:, o, 3:3 + tn], in_=banks[bk][:, 0:tn], func=AF.Copy), reads=[bankb[bk]], writes=[b_xpre])
                        else:
                            act.op(lambda e: e.activation(out=XPS[:, o * NS:(o + 1) * NS, 3:11], in_=banks[bk][:, 0:128].rearrange("p (a b) -> p a b", a=NS), func=AF.Copy),
                                   reads=[bankb[bk]], writes=[b_xps])
                    for o in range(4):
                        bk = o % 2
                        if not samp:
                            pe.op(lambda e: [e.matmul(banks[bk][:, 0:tn], lhsT=DG[:, o, k, :], rhs=XPRE[:, o, k:k + tn], start=(k == 0), stop=(k == 3)) for k in range(4)],
                                  reads=[b_dg, b_xpre], writes=[bankb[bk]])
                        else:
                            pe.op(lambda e: [e.matmul(banks[bk][:, 0:128], lhsT=DG[:, o, k, :], rhs=XPS[:, o * NS:(o + 1) * NS, k:k + 8], start=(k == 0), stop=(k == 3)) for k in range(4)],
                                  reads=[b_dg, b_xps], writes=[bankb[bk]])
                        act.op(lambda e: e.activation(out=XBC[:, o, 0:tn], in_=banks[bk][:, 0:tn], func=AF.Silu, bias=CBA[:, chs[o]:chs[o] + 1]),
                               reads=[bankb[bk], b_cw], writes=[b_xbc[o]])
                    if tt < 3:
                        pool.op(lambda e: e.tensor_copy(out=XPRE[:, :, 0:3], in_=XPRE[:, :, 512:515]), reads=[b_xpre], writes=[b_xpre])
                    if tt >= 3:
                        tl = t0 + tn - 128
                        pe.op(lambda e: [e.matmul(banks[0][:, 0:512], lhsT=HT[:, k, tl:tl + 128], rhs=W[:, k, 256:768], start=(k == 0), stop=(k == KC - 1)) for k in range(KC)],
                              reads=[b_w] + [HTb[k][tt] for k in range(KC)], writes=[bankb[0]])
                        dve.op(lambda e: e.tensor_copy(out=NCV, in_=banks[0][:, 0:512]), reads=[bankb[0]], writes=[b_ncv])
                        if not samp:
                            for (c0, n), wo in zip(ccol, (0, 256, 384)):
                                sp.dma(conv_p[j, :, c0:c0 + n], NCV[125:128, wo:wo + n], reads=[b_ncv])
                        else:
                            pe.op(lambda e: e.matmul(banks[1][0:48, 0:512], lhsT=SEL[:, 16:64], rhs=NCV, start=True, stop=True), reads=[b_con, b_ncv], writes=[bankb[1]])
                            dve.op(lambda e: e.tensor_copy(out=NCV[0:48, :], in_=banks[1][0:48, 0:512]), reads=[bankb[1], b_ncv], writes=[b_ncv])
                            for (c0, n), wo in zip(ccol, (0, 256, 384)):
                                sp.dma(conv_s[j, :, :, c0:c0 + n].rearrange("b k c -> (b k) c"), NCV[0:48, wo:wo + n], reads=[b_ncv])

                    for st_ in range(tn // 128):
                        T = tt * 4 + st_
                        sub = slice(st_ * 128, (st_ + 1) * 128)
                        tok = slice(t0 + st_ * 128, t0 + (st_ + 1) * 128)
                        tri, uu = (TRIp, Up) if not samp else (TRIs, Us)
                        hs = slice(4 * g, 4 * g + 4)
                        hb = lambda arr: sbap(arr, T * 32 + 4 * g, [[1, 4], [0, 64]])
                        hDTS, hDTD, hEE, hCD = hb(DTS), hb(DTD), hb(EE), hb(CD)
                        pe.op(lambda e: [e.matmul(banks[1][:, 0:256], lhsT=HT[:, k, tok], rhs=W[:, k, 0:256], start=(k == 0), stop=(k == KC - 1)) for k in range(KC)],
                              reads=[b_w] + [HTb[k][tt] for k in range(KC)], writes=[bankb[1]])
                        act.op(lambda e: e.activation(out=ZS, in_=banks[1][:, 0:256], func=AF.Silu), reads=[bankb[1]], writes=[b_zs])
                        p2 = pbf(2)
                        pe.op(lambda e: [e.transpose(p2[:, o * 128:(o + 1) * 128], XBC[:, o, sub], identb[:]) for o in range(3)],
                              reads=[b_xbc[0], b_xbc[1], b_xbc[2], b_ident], writes=[bankb[2]])
                        act.op(lambda e: e.activation(out=XSF, in_=p2[:, 0:256], func=AF.Copy), reads=[bankb[2]], writes=[b_xsf])
                        act.op(lambda e: e.activation(out=BTM, in_=p2[:, 256:384], func=AF.Copy), reads=[bankb[2]], writes=[b_btm])
                        dve.op(lambda e: e.tensor_tensor(out=XD, in0=p2[:, 0:256].rearrange("p (a b) -> p a b", a=4), in1=hDTS, op=ALU.mult),
                               reads=[bankb[2], b_dt], writes=[b_xd])
                        dve.op(lambda e: e.tensor_tensor(out=XDD.rearrange("p (a b) -> p a b", a=4), in0=p2[:, 0:256].rearrange("p (a b) -> p a b", a=4), in1=hDTD, op=ALU.mult),
                               reads=[bankb[2], b_dt], writes=[b_xdd])
                        pool.op(lambda e: e.tensor_tensor(out=RSEG, in0=sbap(tri, 0, [[0, 4], [1, 128]]), in1=sbap(AA, T * 32 + 4 * g, [[1, 4], [0, 128]]), op=ALU.mult),
                                reads=[b_con, b_dt], writes=[b_rseg])
                        pe.op(lambda e: e.matmul(banks[3][:], lhsT=uu, rhs=RSEG.rearrange("p a b -> p (a b)"), start=True, stop=True), reads=[b_con, b_rseg], writes=[bankb[3]])
                        act.op(lambda e: e.activation(out=DECT.rearrange("p a b -> p (a b)"), in_=banks[3][:], func=AF.Exp), reads=[bankb[3]], writes=[b_dect])
                        pe.op(lambda e: e.matmul(banks[4][:, 0:128], lhsT=XBC[:, 2, sub], rhs=XBC[:, 3, sub], start=True, stop=True), reads=[b_xbc[2], b_xbc[3]], writes=[bankb[4]])
                        dve.op(lambda e: e.tensor_tensor(out=CBM, in0=banks[4][:, 0:128], in1=tri, op=ALU.mult), reads=[bankb[4], b_con], writes=[b_cbm])
                        pool.op(lambda e: e.tensor_tensor(out=MT, in0=DECT, in1=sbap(CBM, 0, [[0, 4], [1, 128]]), op=ALU.mult), reads=[b_dect, b_cbm], writes=[b_mt])
                        pe.op(lambda e: [e.matmul(banks[5][:, h * 64:(h + 1) * 64], lhsT=MT[:, h, :], rhs=XD[:, h, :], start=True, stop=True) for h in range(4)],
                              reads=[b_mt, b_xd], writes=[bankb[5]])
                        if not samp:
                            pe.op(lambda e: e.matmul(banks[5][:, 256:512], lhsT=XBC[:, 3, sub], rhs=STB, start=True, stop=True), reads=[b_xbc[3], b_stb], writes=[bankb[5]])
                        else:
                            pool.op(lambda e: e.tensor_copy(out=ABC.rearrange("p a (h c) -> p a h c", h=2), in_=sbap(AA, T * 32 + 4 * g, [[2, 2], [1, 2], [0, 64]])),
                                    reads=[b_dt], writes=[b_abc])
                            pe.op(lambda e: [e.matmul(banks[6][:, hp * 16:(hp + 1) * 16], lhsT=ABC[:, hp, :], rhs=SEL[:, 0:16], start=True, stop=True) for hp in range(2)],
                                  reads=[b_abc, b_con], writes=[bankb[6]])
                            act.op(lambda e: e.activation(out=CDT, in_=banks[6][:, 0:32].rearrange("p (a b) -> p a b", a=2), func=AF.Exp), reads=[bankb[6]], writes=[b_cdt])
                            for qb in range(4):
                                for bb in range(4):
                                    b = qb * 4 + bb
                                    b_s0 = b_s0l[bb]
                                    sp.dma(S0[:, bb, :, :], st_ssm[j, b, 4 * g:4 * g + 4].rearrange("(hp hh) p n -> (hh p) hp n", hh=2), writes=[b_s0])
                                    pe.op(lambda e: [e.transpose(banks[7][:, (2 * (bb % 2) + hp) * 128:(2 * (bb % 2) + hp + 1) * 128], S0[:, bb, hp, :], identf[:]) for hp in range(2)],
                                          reads=[b_s0, b_ident], writes=[bankb[7]])
                                    act.op(lambda e: e.activation(out=S0T[:, bb, :], in_=banks[7][:, (2 * (bb % 2)) * 128:(2 * (bb % 2) + 2) * 128], func=AF.Copy),
                                           reads=[bankb[7]], writes=[b_s0t])
                                    pe.op(lambda e: [e.matmul(banks[6][:, 256 + hp * 128 + b * 8:256 + hp * 128 + b * 8 + 8], lhsT=S0T[:, bb, hp * 128:(hp + 1) * 128],
                                                               rhs=XBC[:, 3, b * 8:b * 8 + 8], start=True, stop=True) for hp in range(2)],
                                          reads=[b_s0t, b_xbc[3]], writes=[bankb[6]])
                                    act.op(lambda e: e.activation(out=BMB, in_=BTM, func=AF.Identity, scale=SEL[:, b:b + 1]), reads=[b_btm, b_con], writes=[b_bmb])
                                    pe.op(lambda e: [e.matmul(banks[4][:, 128 + hp * 128:256 + hp * 128], lhsT=XDD[:, hp * 128:(hp + 1) * 128], rhs=BMB, start=True, stop=True) for hp in range(2)],
                                          reads=[b_xdd, b_bmb], writes=[bankb[4]])
                                    for hp in range(2):
                                        dve.op(lambda e: e.scalar_tensor_tensor(out=STG[:, hp, :], in0=S0[:, bb, hp, :], scalar=CDT[:, hp, b:b + 1], in1=banks[4][:, 128 + hp * 128:256 + hp * 128],
                                                                                op0=ALU.mult, op1=ALU.add), reads=[b_s0, b_cdt, bankb[4], b_stg], writes=[b_stg])
                                    sp.dma(ssm_s[j, b, 4 * g:4 * g + 4].rearrange("(hp hh) p n -> (hh p) hp n", hh=2), STG, reads=[b_stg])
                            act.op(lambda e: e.activation(out=YOT, in_=banks[6][:, 256:512].rearrange("p (a b) -> p a b", a=2), func=AF.Copy), reads=[bankb[6]], writes=[b_yot])
                            pe.op(lambda e: [e.transpose(banks[5][:, 256 + hp * 128:256 + (hp + 1) * 128], YOT[:, hp, :], identf[:]) for hp in range(2)],
                                  reads=[b_yot, b_ident], writes=[bankb[5]])
                        dve.op(lambda e: e.tensor_tensor(out=YA.rearrange("p (a b) -> p a b", a=4), in0=banks[5][:, 256:512].rearrange("p (a b) -> p a b", a=4), in1=hEE, op=ALU.mult),
                               reads=[bankb[5], b_dt], writes=[b_ya])
                        dve.op(lambda e: e.tensor_tensor(out=YA, in0=YA, in1=banks[5][:, 0:256], op=ALU.add), reads=[bankb[5], b_ya], writes=[b_ya])
                        dve.op(lambda e: e.tensor_tensor(out=YB.rearrange("p (a b) -> p a b", a=4), in0=XSF.rearrange("p (a b) -> p a b", a=4),
                                                         in1=sbap(ROW, 2 * 32 + 4 * g, [[1, 4], [0, 64]]), op=ALU.mult), reads=[b_xsf, b_row], writes=[b_yb])
                        dve.op(lambda e: e.tensor_tensor(out=YA, in0=YA, in1=YB, op=ALU.add), reads=[b_ya, b_yb], writes=[b_ya])
                        dve.op(lambda e: e.tensor_tensor(out=YA, in0=YA, in1=ZS, op=ALU.mult), reads=[b_ya, b_zs], writes=[b_ya])
                        pool.op(lambda e: e.memset(SS[:, 0:1], 0.0), writes=[b_ss])
                        act.op(lambda e: e.activation(out=YB, in_=YA, func=AF.Square, accum_out=SS[:, 0:1]), reads=[b_ya, b_yb, b_ss], writes=[b_yb, b_ss])
                        dve.op(lambda e: e.tensor_scalar(out=SS[:, 1:2], in0=SS[:, 0:1], scalar1=1.0 / 256.0, scalar2=EPS, op0=ALU.mult, op1=ALU.add), reads=[b_ss], writes=[b_ss])
                        act.op(lambda e: e.activation(out=SS[:, 1:2], in_=SS[:, 1:2], func=AF.Ln), reads=[b_ss], writes=[b_ss])
                        act.op(lambda e: e.activation(out=SS[:, 1:2], in_=SS[:, 1:2], func=AF.Exp, scale=-0.5), reads=[b_ss], writes=[b_ss])
                        dve.op(lambda e: e.scalar_tensor_tensor(out=YN, in0=YA, scalar=SS[:, 1:2], in1=NW, op0=ALU.mult, op1=ALU.mult), reads=[b_ya, b_ss, b_nw], writes=[b_yn])
                        pe.op(lambda e: [e.transpose(p2[:, 512 + c2 * 128:512 + (c2 + 1) * 128], YN[:, c2 * 128:(c2 + 1) * 128], identb[:]) for c2 in range(2)],
                              reads=[b_yn, b_ident], writes=[bankb[2]])
                        act.op(lambda e: e.activation(out=YT[:, :, sub], in_=p2[:, 512:768].rearrange("p (a b) -> p a b", a=2), func=AF.Copy), reads=[bankb[2]], writes=[b_yt])
                        if not samp:
                            pe.op(lambda e: e.matmul(banks[6][:, 0:256], lhsT=BTM, rhs=XDD, start=True, stop=True), reads=[b_btm, b_xdd], writes=[bankb[6]])
                            dve.op(lambda e: e.tensor_tensor(out=ST.rearrange("p (a b) -> p a b", a=4), in0=ST.rearrange("p (a b) -> p a b", a=4), in1=hCD, op=ALU.mult),
                                   reads=[b_st, b_dt], writes=[b_st])
                            dve.op(lambda e: e.tensor_tensor(out=ST, in0=ST, in1=banks[6][:, 0:256], op=ALU.add), reads=[b_st, bankb[6]], writes=[b_st])
                            act.op(lambda e: e.activation(out=STB, in_=ST, func=AF.Copy), reads=[b_st], writes=[b_stb])
                            if T == 15:
                                pe.op(lambda e: [e.transpose(banks[7][:, hp * 128:(hp + 1) * 128], ST[:, hp * 128:(hp + 1) * 128], identf[:]) for hp in range(2)],
                                      reads=[b_st, b_ident], writes=[bankb[7]])
                                dve.op(lambda e: e.tensor_copy(out=STG, in_=banks[7][:, 0:256].rearrange("p (a b) -> p a b", a=2)), reads=[bankb[7], b_stg], writes=[b_stg])
                                sp.dma(ssm_p[j, 4 * g:4 * g + 4].rearrange("(hp hh) p n -> (hh p) hp n", hh=2), STG, reads=[b_stg])
                    for m in range(KC):
                        pe.op(lambda e: [e.matmul(banks[7][:, 0:tn], lhsT=OW[:, c2, m * 128:(m + 1) * 128], rhs=YT[:, c2, 0:tn], start=(c2 == 0), stop=(c2 == 1)) for c2 in range(2)],
                              reads=[b_ow, b_yt], writes=[bankb[7]])
                        resid_add(m, tt, 7, 16)

        for l in range(DEPTH if STAGE != "ssd1" else 1):
            compute_mod(l)
            if l % 2 == 1 and STAGE in ("full", "s5"):
                modulate(0, 8)
                s5_layer(l, l // 2)
            if l % 2 == 0 and STAGE in ("full", "ssd", "ssd1"):
                modulate(0, 8)
                ssd_layer(l, l // 2)
            if STAGE == "ssd1":
                continue
            modulate(24, 32)
            ffn(l)

        new_phase()
        yst = [ar32.take(D) for i in range(2)]
        ystb = [Buf(f"yst{i}") for i in range(2)]
        for ti in range(5):
            t0, tn = TT[ti]
            rms_rstd(ti)
            for k in range(KC):
                dve.op(lambda e, k=k: e.scalar_tensor_tensor(out=XT[:, k, t0:t0 + tn], in0=XT[:, k, t0:t0 + tn], scalar=fnw[:, k:k + 1],
                                                             in1=rstd[:, 0:tn], op0=ALU.mult, op1=ALU.mult),
                       reads=[XTb[k][ti], b_rstd, b_fnw], writes=[XTb[k][ti]])
        for tb in range(NT // 128):
            s = tb % 2
            ti = min(tb // 4, 4)
            for half in range(2):
                bk = half
                def tr(e, tb=tb, half=half, bk=bk):
                    return [e.transpose(banks[bk][:, q * 128:(q + 1) * 128], XT[:, half * 4 + q, tb * 128:(tb + 1) * 128], identf[:])
                            for q in range(4)]
                pe.op(tr, reads=[XTb[half * 4 + q][ti] for q in range(4)] + [b_ident], writes=[bankb[bk]])
                if half == 0:
                    act.op(lambda e, s=s, bk=bk: e.activation(out=yst[s][:, 0:512], in_=banks[bk][:], func=AF.Copy), reads=[bankb[bk]], writes=[ystb[s]])
                else:
                    dve.op(lambda e, s=s, bk=bk: e.tensor_copy(out=yst[s][:, 512:1024], in_=banks[bk][:]), reads=[bankb[bk]], writes=[ystb[s]])
            sp.dma(yout[tb * 128:(tb + 1) * 128, :], yst[s], reads=[ystb[s]])

        fw.finish()
        blk = st.enter_context(nc.Block())
        fw.emit_all(blk)
    return nc


_CONSTS = None


def _consts():
    c = np.zeros((8, 128, 128), np.float32)
    idx = np.arange(128)
    c[0] = np.eye(128)
    c[1] = (idx[:, None] <= idx[None, :])
    c[2] = (idx[:, None] > idx[None, :])
    c[3] = 1.0
    same = (idx[:, None] // 8) == (idx[None, :] // 8)
    c[4] = c[1] * same
    c[5] = c[2] * same
    c[6] = same
    for b in range(16):
        c[7, 8 * b:8 * b + 8, b] = 1.0
        for k in range(3):
            c[7, 8 * b + 5 + k, 16 + 3 * b + k] = 1.0
    return c


def _s5_layouts(inp):
    f32 = np.float32
    Bre, Bim = np.asarray(inp["s5_B_re"], f32), np.asarray(inp["s5_B_im"], f32)
    Cre, Cim = np.asarray(inp["s5_C_re"], f32), np.asarray(inp["s5_C_im"], f32)
    BT = np.zeros((2, 8, 4, 2, 16, 2, 4, 2, 64), f32)
    CT = np.zeros((2, 32, 2, 64, 2, 4, 2, 16), f32)
    for ri, (B_, C_) in enumerate(((Bre, Cre), (Bim, Cim))):
        Bv = B_.reshape(2, 8, 4, 2, 64, 16)
        Cv = C_.reshape(2, 32, 2, 16, 64)
        for pi in range(4):
            for gl in range(2):
                BT[:, :, pi, gl, :, ri, pi, gl, :] = Bv[:, :, pi, gl].transpose(0, 1, 3, 2)
        for q in range(32):
            for gl in range(2):
                CT[:, q, gl, :, ri, q % 4, gl, :] = Cv[:, q, gl].transpose(0, 2, 1)
    BT = np.ascontiguousarray(BT.reshape(2, 8, 128, 1024))
    CT = np.ascontiguousarray(CT.reshape(2, 32, 128, 256))
    Are = np.asarray(inp["s5_A_re"], f32).reshape(2, 32, 128).transpose(0, 2, 1)
    Aim = np.asarray(inp["s5_A_im"], f32).reshape(2, 32, 128).transpose(0, 2, 1)
    ldt = np.repeat(np.asarray(inp["s5_log_dt"], f32).reshape(2, 32, 2, 1), 64, axis=3).reshape(2, 32, 128).transpose(0, 2, 1)
    col = np.ascontiguousarray(np.stack([Are, Aim, ldt], axis=2).reshape(2, 128, 96))
    Dt = np.ascontiguousarray(np.asarray(inp["s5_D"], f32).reshape(2, KC, 128).transpose(0, 2, 1))
    return BT, CT, col, Dt


def kernel(**inp):
    f32 = np.float32
    g = lambda k: np.asarray(inp[k], dtype=f32)
    x_prompt, x_sample = g("x_prompt"), g("x_sample")
    c_prompt, c_sample = g("c_prompt"), g("c_sample")
    nc = build_program()
    consts = _consts()
    ada_bT = np.ascontiguousarray(g("ada_b").reshape(DEPTH, 48, 128).transpose(0, 2, 1))
    fnwT = np.ascontiguousarray(g("final_norm_w").reshape(KC, 128).T)
    BT, CT, col, Dt = _s5_layouts(inp)
    tidx = np.concatenate([np.arange(TP), np.tile(np.arange(TS), NS)]).astype(f32).reshape(1, NT)
    shared = dict(consts=consts, ada_w=g("ada_w"), ada_bT=ada_bT, ffn_w_in=g("ffn_w_in"), ffn_w_out=g("ffn_w_out"), fnwT=fnwT,
                  tidx=tidx, s5BT=BT, s5CT=CT, s5col=col, s5Dt=Dt, s5_glu_w=g("s5_glu_w"),
                  ssd_in_w=g("ssd_in_w"), ssd_out_w=g("ssd_out_w"),
                  ssd_cwT=np.ascontiguousarray(g("ssd_conv_w").reshape(2, 4, 32, 128).transpose(0, 3, 2, 1)),
                  ssd_cbT=np.ascontiguousarray(g("ssd_conv_b").reshape(2, 32, 128).transpose(0, 2, 1)),
                  ssd_row=np.ascontiguousarray(np.stack([g("ssd_dt_bias"), g("ssd_A_log"), g("ssd_D")], 1)),
                  ssd_norm_w=g("ssd_norm_w"))
    st_ssm_all, st_conv_all = g("state_ssm"), g("state_conv")
    s5re, s5im = g("state_s5_re"), g("state_s5_im")
    for k in SHRINK:
        shared[k] = np.zeros([1] * shared[k].ndim, f32)
    in_maps = []
    for c in range(NCORES):
        m = dict(shared)
        m["xin"] = np.ascontiguousarray(np.concatenate([x_prompt[c], x_sample[c * NS:(c + 1) * NS].reshape(NS * TS, D)], 0))
        m["cin"] = np.ascontiguousarray(np.concatenate([c_prompt[c:c + 1], c_sample[c * NS:(c + 1) * NS]], 0))
        x0 = np.stack([s5re[:, c * NS:(c + 1) * NS], s5im[:, c * NS:(c + 1) * NS]], 1)
        x0 = x0.reshape(2, 2, NS, 32, 128).transpose(0, 1, 4, 3, 2)
        m["s5x0"] = np.ascontiguousarray(x0.reshape(2, 2, 128, 512))
        m["st_ssm"] = np.ascontiguousarray(st_ssm_all[:, c * NS:(c + 1) * NS])
        m["st_conv"] = np.ascontiguousarray(st_conv_all[:, c * NS:(c + 1) * NS])
        in_maps.append(m)
    res = run_bass_kernel_spmd(nc, in_maps, core_ids=list(range(NCORES)))
    R = res.results
    y_prompt = np.stack([R[c]["yout"][:TP] for c in range(NCORES)], 0)
    y_sample = np.concatenate([R[c]["yout"][TP:].reshape(NS, TS, D) for c in range(NCORES)], 0)
    s5o = np.stack([R[c]["s5out"] for c in range(NCORES)], 0)
    re_p = s5o[:, :, 0, 0].transpose(1, 0, 2).reshape(2, NCORES, 64, 64)
    im_p = s5o[:, :, 1, 0].transpose(1, 0, 2).reshape(2, NCORES, 64, 64)
    re_s = s5o[:, :, 0, 1:].transpose(1, 0, 2, 3).reshape(2, NCORES * NS, 64, 64)
    im_s = s5o[:, :, 1, 1:].transpose(1, 0, 2, 3).reshape(2, NCORES * NS, 64, 64)
    ssm_p = np.stack([R[c]["ssm_p"] for c in range(NCORES)], 1)
    conv_p = np.stack([R[c]["conv_p"] for c in range(NCORES)], 1)
    ssm_s = np.concatenate([R[c]["ssm_s"] for c in range(NCORES)], 1)
    conv_s = np.concatenate([R[c]["conv_s"] for c in range(NCORES)], 1)
    c_ = np.ascontiguousarray
    return (y_prompt, y_sample, ssm_p, conv_p, c_(re_p), c_(im_p), ssm_s, conv_s, c_(re_s), c_(im_s))
```
